# Optimizing a Trainium2 kernel written in Bass

```python
import jax, jax.numpy as jnp
from jax import lax
import numpy as np

D_MODEL = 1024
BATCH = 8
SEQ = 4096
DEPTH = 4

N_MIXERS = 3
D_FF = 2816
NORM_EPS = 1e-5

GLA_HEADS = 4
GLA_DK = D_MODEL // 2
GLA_DV = D_MODEL
GLA_HEAD_K = GLA_DK // GLA_HEADS
GLA_HEAD_V = GLA_DV // GLA_HEADS
GLA_GATE_RANK = 16
GLA_GATE_NORMALIZER = 16.0
GLA_CHUNK = 64
GLA_IN = 2 * GLA_DK + 2 * GLA_DV + GLA_GATE_RANK

SGU_D_FFN = 6 * D_MODEL
SGU_HALF = SGU_D_FFN // 2
SGU_GROUPS = 8
SGU_GROUP_DIM = SGU_HALF // SGU_GROUPS
SGU_CHUNK = 128

SWA_HEADS = 16
SWA_KV_HEADS = 2
SWA_HEAD_DIM = 64
SWA_GROUP = SWA_HEADS // SWA_KV_HEADS
SWA_WINDOW = 128
SWA_BLOCK = 128
SWA_QKV = (SWA_HEADS + 2 * SWA_KV_HEADS) * SWA_HEAD_DIM
ROPE_DIM = SWA_HEAD_DIM // 4
ROPE_THETA = 500000.0

kernel_name = 'hybrid_gla_sgu_swa_macaron'


def rms_norm(x, gain):
    xf = x.astype(jnp.float32)
    y = xf * lax.rsqrt(jnp.mean(xf * xf, axis=-1, keepdims=True) + NORM_EPS)
    return (y * gain.astype(jnp.float32)).astype(x.dtype)


def layer_norm(x, gain, bias):
    xf = x.astype(jnp.float32)
    mu = jnp.mean(xf, axis=-1, keepdims=True)
    xc = xf - mu
    var = jnp.mean(xc * xc, axis=-1, keepdims=True)
    y = xc * lax.rsqrt(var + NORM_EPS) * gain.astype(jnp.float32) + bias.astype(jnp.float32)
    return y.astype(x.dtype)


def swiglu_ffn(h, w_in, w_out):
    gate, up = jnp.split(h @ w_in, 2, axis=-1)
    return (jax.nn.silu(gate) * up) @ w_out


def gla_mixer(h, w_in, w_gk_up, b_gk, o_norm, w_out):
    B, S, _ = h.shape
    f32 = jnp.float32
    q, k, v, r, gk_low = jnp.split(h @ w_in, [GLA_DK, 2 * GLA_DK, 2 * GLA_DK + GLA_DV, 2 * GLA_DK + 2 * GLA_DV], axis=-1)
    log_a = jax.nn.log_sigmoid((gk_low @ w_gk_up + b_gk).astype(f32)) / GLA_GATE_NORMALIZER
    n = S // GLA_CHUNK

    def to_chunks(t, dh):
        return t.reshape(B, n, GLA_CHUNK, GLA_HEADS, dh).transpose(1, 0, 3, 2, 4).astype(f32)

    q = to_chunks(q, GLA_HEAD_K) * (GLA_HEAD_K ** -0.5)
    k = to_chunks(k, GLA_HEAD_K)
    v = to_chunks(v, GLA_HEAD_V)
    b = jnp.cumsum(to_chunks(log_a, GLA_HEAD_K), axis=3)
    b_last = b[:, :, :, -1:, :]
    q_dec = q * jnp.exp(b)
    k_intra = k * jnp.exp(-b)
    k_state = k * jnp.exp(b_last - b)
    causal = jnp.tril(jnp.ones((GLA_CHUNK, GLA_CHUNK), dtype=bool))
    attn = jnp.where(causal, jnp.einsum('nbhik,nbhjk->nbhij', q_dec, k_intra), 0.0)
    o_intra = jnp.einsum('nbhij,nbhjv->nbhiv', attn, v)

    def step(state, xs):
        q_c, k_c, v_c, decay_c = xs
        o_inter = jnp.einsum('bhik,bhkv->bhiv', q_c, state)
        state = state * decay_c[:, :, 0, :, None] + jnp.einsum('bhjk,bhjv->bhkv', k_c, v_c)
        return state, o_inter

    state0 = jnp.zeros((B, GLA_HEADS, GLA_HEAD_K, GLA_HEAD_V), f32)
    _, o_inter = lax.scan(step, state0, (q_dec, k_state, v, jnp.exp(b_last)))
    o = (o_intra + o_inter).transpose(1, 0, 3, 2, 4).reshape(B, S, GLA_HEADS, GLA_HEAD_V)
    o = rms_norm(o, o_norm).reshape(B, S, GLA_DV) * jax.nn.silu(r.astype(f32))
    return o.astype(h.dtype) @ w_out


def sgu_mixer(h, w_in, ln_gain, ln_bias, w_s, b_s, w_out):
    B, S, _ = h.shape
    u, v = jnp.split(jax.nn.gelu(h @ w_in, approximate=False), 2, axis=-1)
    v = layer_norm(v, ln_gain, ln_bias)
    n = S // SGU_CHUNK
    v = v.reshape(B, n, SGU_CHUNK, SGU_GROUPS, SGU_GROUP_DIM)
    causal = jnp.tril(jnp.ones((SGU_CHUNK, SGU_CHUNK), dtype=bool))
    w_causal = jnp.where(causal[None], w_s, 0.0)
    v = jnp.einsum('gij,bnjgd->bnigd', w_causal.astype(v.dtype), v) + b_s.T[None, None, :, :, None]
    return (u * v.reshape(B, S, SGU_HALF)) @ w_out


def rope_tables(positions):
    inv_freq = ROPE_THETA ** (-jnp.arange(0, ROPE_DIM, 2, dtype=jnp.float32) / ROPE_DIM)
    ang = positions.astype(jnp.float32)[..., None] * inv_freq
    return jnp.cos(ang)[:, :, None, :], jnp.sin(ang)[:, :, None, :]


def rope_partial(t, cos, sin):
    cos = cos.astype(t.dtype)
    sin = sin.astype(t.dtype)
    half = ROPE_DIM // 2
    x1, x2, rest = t[..., :half], t[..., half:ROPE_DIM], t[..., ROPE_DIM:]
    return jnp.concatenate([x1 * cos - x2 * sin, x2 * cos + x1 * sin, rest], axis=-1)


def swa_mixer(h, cos, sin, w_qkv, b_qkv, sinks, w_out, b_out):
    B, S, _ = h.shape
    HD = SWA_HEAD_DIM
    q, k, v = jnp.split(h @ w_qkv + b_qkv, [SWA_HEADS * HD, (SWA_HEADS + SWA_KV_HEADS) * HD], axis=-1)
    q = rope_partial(q.reshape(B, S, SWA_HEADS, HD), cos, sin)
    k = rope_partial(k.reshape(B, S, SWA_KV_HEADS, HD), cos, sin)
    v = v.reshape(B, S, SWA_KV_HEADS, HD)
    n = S // SWA_BLOCK
    qb = q.reshape(B, n, SWA_BLOCK, SWA_KV_HEADS, SWA_GROUP, HD)
    kb = k.reshape(B, n, SWA_BLOCK, SWA_KV_HEADS, HD)
    vb = v.reshape(B, n, SWA_BLOCK, SWA_KV_HEADS, HD)

    def band(t):
        prev = jnp.concatenate([jnp.zeros_like(t[:, :1]), t[:, :-1]], axis=1)
        return jnp.concatenate([prev, t], axis=2)

    k_band, v_band = band(kb), band(vb)
    scores = jnp.einsum('bnqkgd,bnskd->bnkgqs', qb, k_band).astype(jnp.float32) * (HD ** -0.5)
    qi = jnp.arange(SWA_BLOCK)[:, None] + SWA_BLOCK
    kj = jnp.arange(2 * SWA_BLOCK)[None, :]
    delta = qi - kj
    in_window = (delta >= 0) & (delta < SWA_WINDOW)
    not_pad = (jnp.arange(n)[:, None, None] > 0) | (kj >= SWA_BLOCK)[None]
    valid = in_window[None] & not_pad
    scores = jnp.where(valid[None, :, None, None], scores, -jnp.inf)
    sink = jnp.broadcast_to(sinks.astype(jnp.float32).reshape(1, 1, SWA_KV_HEADS, SWA_GROUP, 1, 1), scores.shape[:-1] + (1,))
    probs = jax.nn.softmax(jnp.concatenate([scores, sink], axis=-1), axis=-1)[..., :-1]
    out = jnp.einsum('bnkgqs,bnskd->bnqkgd', probs.astype(v.dtype), v_band)
    return out.reshape(B, S, SWA_HEADS * HD) @ w_out + b_out


def setup_inputs(seed: int = 0) -> dict:
    key = jax.random.key(seed)
    keys = iter(jax.random.split(key, 128))

    def normal(shape, scale):
        return scale * jax.random.normal(next(keys), shape, jnp.float32)

    def gain(dim):
        return 1.0 + normal((dim,), 0.02)

    inp = {}
    inp['x'] = normal((BATCH, SEQ, D_MODEL), 1.0)
    inp['positions'] = jnp.broadcast_to(jnp.arange(SEQ, dtype=jnp.int32), (BATCH, SEQ))

    def ffn(prefix):
        inp[prefix + '_norm'] = gain(D_MODEL)
        inp[prefix + '_w_in'] = normal((D_MODEL, 2 * D_FF), D_MODEL ** -0.5)
        inp[prefix + '_w_out'] = normal((D_FF, D_MODEL), D_FF ** -0.5)

    for layer in range(DEPTH):
        p = 'l%d' % layer
        ffn(p + '_ffn1')
        inp[p + '_mix_norm'] = gain(D_MODEL)
        kind = layer % N_MIXERS
        if kind == 0:
            inp[p + '_gla_w_in'] = normal((D_MODEL, GLA_IN), D_MODEL ** -0.5)
            inp[p + '_gla_w_gk_up'] = normal((GLA_GATE_RANK, GLA_DK), GLA_GATE_RANK ** -0.5)
            inp[p + '_gla_b_gk'] = normal((GLA_DK,), 0.02)
            inp[p + '_gla_o_norm'] = gain(GLA_HEAD_V)
            inp[p + '_gla_w_out'] = normal((GLA_DV, D_MODEL), GLA_DV ** -0.5)
        elif kind == 1:
            inp[p + '_sgu_w_in'] = normal((D_MODEL, SGU_D_FFN), D_MODEL ** -0.5)
            inp[p + '_sgu_ln_gain'] = gain(SGU_HALF)
            inp[p + '_sgu_ln_bias'] = normal((SGU_HALF,), 0.02)
            inp[p + '_sgu_w_s'] = normal((SGU_GROUPS, SGU_CHUNK, SGU_CHUNK), SGU_CHUNK ** -0.5)
            inp[p + '_sgu_b_s'] = 1.0 + normal((SGU_GROUPS, SGU_CHUNK), 0.02)
            inp[p + '_sgu_w_out'] = normal((SGU_HALF, D_MODEL), SGU_HALF ** -0.5)
        else:
            inp[p + '_swa_w_qkv'] = normal((D_MODEL, SWA_QKV), D_MODEL ** -0.5)
            inp[p + '_swa_b_qkv'] = normal((SWA_QKV,), 0.02)
            inp[p + '_swa_sinks'] = normal((SWA_HEADS,), 0.5)
            inp[p + '_swa_w_out'] = normal((SWA_HEADS * SWA_HEAD_DIM, D_MODEL), (SWA_HEADS * SWA_HEAD_DIM) ** -0.5)
            inp[p + '_swa_b_out'] = normal((D_MODEL,), 0.02)
        ffn(p + '_ffn2')
    inp['final_norm'] = gain(D_MODEL)
    return inp


def reference(x, positions,
              l0_ffn1_norm, l0_ffn1_w_in, l0_ffn1_w_out, l0_mix_norm,
              l0_gla_w_in, l0_gla_w_gk_up, l0_gla_b_gk, l0_gla_o_norm, l0_gla_w_out,
              l0_ffn2_norm, l0_ffn2_w_in, l0_ffn2_w_out,
              l1_ffn1_norm, l1_ffn1_w_in, l1_ffn1_w_out, l1_mix_norm,
              l1_sgu_w_in, l1_sgu_ln_gain, l1_sgu_ln_bias, l1_sgu_w_s, l1_sgu_b_s, l1_sgu_w_out,
              l1_ffn2_norm, l1_ffn2_w_in, l1_ffn2_w_out,
              l2_ffn1_norm, l2_ffn1_w_in, l2_ffn1_w_out, l2_mix_norm,
              l2_swa_w_qkv, l2_swa_b_qkv, l2_swa_sinks, l2_swa_w_out, l2_swa_b_out,
              l2_ffn2_norm, l2_ffn2_w_in, l2_ffn2_w_out,
              l3_ffn1_norm, l3_ffn1_w_in, l3_ffn1_w_out, l3_mix_norm,
              l3_gla_w_in, l3_gla_w_gk_up, l3_gla_b_gk, l3_gla_o_norm, l3_gla_w_out,
              l3_ffn2_norm, l3_ffn2_w_in, l3_ffn2_w_out,
              final_norm):
    cos, sin = rope_tables(positions)
    ffn1 = [(l0_ffn1_norm, l0_ffn1_w_in, l0_ffn1_w_out), (l1_ffn1_norm, l1_ffn1_w_in, l1_ffn1_w_out),
            (l2_ffn1_norm, l2_ffn1_w_in, l2_ffn1_w_out), (l3_ffn1_norm, l3_ffn1_w_in, l3_ffn1_w_out)]
    ffn2 = [(l0_ffn2_norm, l0_ffn2_w_in, l0_ffn2_w_out), (l1_ffn2_norm, l1_ffn2_w_in, l1_ffn2_w_out),
            (l2_ffn2_norm, l2_ffn2_w_in, l2_ffn2_w_out), (l3_ffn2_norm, l3_ffn2_w_in, l3_ffn2_w_out)]
    mix_norms = [l0_mix_norm, l1_mix_norm, l2_mix_norm, l3_mix_norm]
    mixers = [
        lambda t: gla_mixer(t, l0_gla_w_in, l0_gla_w_gk_up, l0_gla_b_gk, l0_gla_o_norm, l0_gla_w_out),
        lambda t: sgu_mixer(t, l1_sgu_w_in, l1_sgu_ln_gain, l1_sgu_ln_bias, l1_sgu_w_s, l1_sgu_b_s, l1_sgu_w_out),
        lambda t: swa_mixer(t, cos, sin, l2_swa_w_qkv, l2_swa_b_qkv, l2_swa_sinks, l2_swa_w_out, l2_swa_b_out),
        lambda t: gla_mixer(t, l3_gla_w_in, l3_gla_w_gk_up, l3_gla_b_gk, l3_gla_o_norm, l3_gla_w_out),
    ]
    for i in range(DEPTH):
        n1, wi1, wo1 = ffn1[i]
        n2, wi2, wo2 = ffn2[i]
        x = x + 0.5 * swiglu_ffn(rms_norm(x, n1), wi1, wo1)
        x = x + mixers[i](rms_norm(x, mix_norms[i]))
        x = x + 0.5 * swiglu_ffn(rms_norm(x, n2), wi2, wo2)
    return rms_norm(x, final_norm)
```

```python
from contextlib import ExitStack
import math
import os
import numpy as np
import concourse.bass as bass
import concourse.mybir as mybir
from concourse.bass_utils import run_bass_kernel_spmd

F32 = mybir.dt.float32
BF16 = mybir.dt.bfloat16
I32 = mybir.dt.int32
AF = mybir.ActivationFunctionType
ALU = mybir.AluOpType
AX = mybir.AxisListType

D = 1024
SEQ = 4096
NCH = 8
DFF = 2816
EPS = 1e-5
N_CORES = 8


class Buf:
    __slots__ = ("name", "w", "r")

    def __init__(self, name=""):
        self.name = name
        self.w = None
        self.r = []


class Q:
    def __init__(self, name, sem, key, same):
        self.name = name
        self.sem = sem
        self.key = key
        self.n = 0
        self.items = []
        self.waited = {}
        self.same = same


class Sched:
    def __init__(self, nc, stack):
        self.nc = nc
        self.stack = stack
        self.sems = {}
        self.q = {}
        for name in ("pe", "act", "dve", "pool", "sp"):
            sem = stack.enter_context(nc.semaphore("q_" + name))
            self.sems["q_" + name] = sem
            self.q[name] = Q(name, sem, "q_" + name, name in ("act", "dve", "pool"))
        self.dma_sems = {}

    def dma_sem(self, name):
        if name not in self.dma_sems:
            sem = self.stack.enter_context(self.nc.semaphore("d_" + name))
            self.sems["d_" + name] = sem
            self.dma_sems[name] = [sem, 0, "d_" + name]
        return self.dma_sems[name]

    def _deps(self, q, reads, writes):
        deps = {}

        def add(tok, raw):
            if tok is None:
                return
            k, v = tok
            if k == q.key and not q.same:
                return
            if deps.get(k, 0) < v:
                deps[k] = v
        for b in reads:
            add(b.w, True)
        for b in writes:
            add(b.w, True)
            for t in b.r:
                add(t, False)
        waits = []
        for k, v in deps.items():
            if q.waited.get(k, 0) < v:
                q.waited[k] = v
                waits.append((self.sems[k], v))
        return waits

    def _commit(self, tok, reads, writes):
        for b in writes:
            b.w = tok
            b.r = []
        for b in reads:
            b.r.append(tok)

    def op(self, eng, fn, reads=(), writes=()):
        q = self.q[eng]
        waits = self._deps(q, reads, writes)
        q.n += 1
        tok = (q.key, q.n)
        q.items.append((waits, fn, (q.sem, 1)))
        self._commit(tok, reads, writes)
        return tok

    def dma(self, eng, semname, fn, reads=(), writes=()):
        q = self.q[eng]
        waits = self._deps(q, reads, writes)
        ent = self.dma_sem(semname)
        ent[1] += 16
        tok = (ent[2], ent[1])
        q.items.append((waits, fn, (ent[0], 16)))
        self._commit(tok, reads, writes)
        return tok

    def barrier(self, engs=("pe", "act", "dve", "pool"), waiters=("pe", "act", "dve", "pool", "sp")):
        for e in waiters:
            q = self.q[e]
            waits = []
            for o in engs:
                if o == e:
                    continue
                oq = self.q[o]
                if oq.n > 0 and q.waited.get(oq.key, 0) < oq.n:
                    q.waited[oq.key] = oq.n
                    waits.append((oq.sem, oq.n))
            if waits:
                q.items.append((waits, None, None))

    def wait_bufs(self, eng, bufs):
        q = self.q[eng]
        waits = self._deps(q, bufs, ())
        if waits:
            q.items.append((waits, None, None))

    def final_wait(self, eng, toks):
        q = self.q[eng]
        q.items.append(([(self.sems[k], v) for (k, v) in toks], None, None))

    def run(self):
        nc = self.nc
        with nc.Block() as block:
            def replay(q):
                def body(e):
                    for waits, fn, inc in q.items:
                        for (s, v) in waits:
                            e.wait_ge(s, v)
                        if fn is not None:
                            ins = fn(e)
                            if inc is not None:
                                ins.then_inc(inc[0], inc[1])
                return body
            block.tensor(replay(self.q["pe"]))
            block.scalar(replay(self.q["act"]))
            block.vector(replay(self.q["dve"]))
            block.gpsimd(replay(self.q["pool"]))
            block.sync(replay(self.q["sp"]))


class Arena:
    def __init__(self, t, words):
        self.t = t
        self.words = words
        self.off = 0

    def reset(self, off=0):
        self.off = off

    def f32(self, n):
        assert self.off + n <= self.words, ("arena overflow", self.off, n, self.words)
        v = self.t[:, self.off:self.off + n]
        self.off += n
        return v

    def bf16(self, n):
        assert n % 2 == 0
        return self.f32(n // 2).bitcast(BF16)

    def i32(self, n):
        return self.f32(n).bitcast(I32)


class Prog:
    def __init__(self, plan, final_norm, weight_shapes):
        self.plan = plan
        self.final_norm = final_norm
        self.nc = bass.Bass("TRN2", target_bir_lowering=False)
        nc = self.nc
        self.dram = {}
        self.dram["x"] = nc.dram_tensor("x", [SEQ, D], F32, kind="ExternalInput").ap()
        self.dram["pos"] = nc.dram_tensor("pos", [SEQ], I32, kind="ExternalInput").ap()
        for name, shp in weight_shapes.items():
            self.dram[name] = nc.dram_tensor(name, list(shp), F32, kind="ExternalInput").ap()
        self.dram["y"] = nc.dram_tensor("y", [SEQ, D], F32, kind="ExternalOutput").ap()

    def build(self):
        nc = self.nc
        with ExitStack() as st:
            self.st = st
            S = self.S = Sched(nc, st)
            self.X = st.enter_context(nc.sbuf_tensor("XT", [128, NCH, SEQ], F32))
            self.xb = [[Buf("X%d_%d" % (c, t)) for t in range(8)] for c in range(NCH)]
            AW = 20300
            art = st.enter_context(nc.sbuf_tensor("arena", [128, AW], F32))
            self.ar = Arena(art, AW)
            self.ps = st.enter_context(nc.psum_tensor("ps", [128, 8, 512], F32))
            self.pb = [Buf("bank%d" % i) for i in range(8)]
            self.consts()
            self.load_x()
            for item in self.plan:
                S.barrier()
                self.ar.reset(self.ar_base)
                getattr(self, "emit_" + item[0])(*item[1:])
            S.barrier()
            self.ar.reset(self.ar_base)
            self.store_y()
            S.run()
        return nc

    def bank(self, i):
        return self.ps[:, i, :]

    def xall(self, t0, t1):
        return [self.xb[c][t] for c in range(NCH) for t in range(t0, t1)]

    def consts(self):
        nc, S, ar = self.nc, self.S, self.ar
        self.ones_bf = ar.bf16(128)
        self.ident = ar.f32(128)
        self.ident_bf = ar.bf16(128)
        self.b_ones = Buf("ones")
        self.b_ident = Buf("ident")
        tmp1 = self.ones_f32 = ar.f32(128)
        b_tmp = self.b_ones32 = Buf("tmp1")
        S.op("dve", lambda e: e.memset(self.ones_bf, 1.0), writes=[self.b_ones])
        S.op("pool", lambda e: e.memset(tmp1, 1.0), writes=[b_tmp])
        S.op("pool", lambda e: e.affine_select(out=self.ident, in_=tmp1, pattern=[[1, 128]],
                                               compare_op=ALU.is_equal, fill=0.0, base=0,
                                               channel_multiplier=-1),
             reads=[b_tmp], writes=[self.b_ident])
        S.op("dve", lambda e: e.tensor_copy(out=self.ident_bf, in_=self.ident),
             reads=[self.b_ident], writes=[self.b_ident])
        vecs = []
        for item in self.plan:
            kind = item[0]
            if kind == "ffn":
                vecs.append((item[1] + "_norm", D))
            else:
                lp = item[1]
                vecs.append((lp + "_mix_norm", D))
                if kind == "gla":
                    vecs.append((lp + "_gla_b_gk", 512))
                elif kind == "sgu":
                    vecs.append((lp + "_sgu_ln_gain", 3072))
                    vecs.append((lp + "_sgu_ln_bias", 3072))
                elif kind == "swa":
                    vecs.append((lp + "_swa_b_out", D))
        self.vec_col = {}
        ncols = sum(n // 128 for _, n in vecs)
        self.vecs_sb = ar.f32(max(ncols, 1))
        self.b_vecs = Buf("vecs")
        col = 0
        groups = []
        cur = []
        curp = 0
        for name, n in vecs:
            r = n // 128
            if curp + r > 128:
                groups.append(cur)
                cur, curp = [], 0
            cur.append((name, r, curp, col))
            self.vec_col[name] = col
            curp += r
            col += r
        if cur:
            groups.append(cur)
        stage = ar.f32(128)
        b_stage = Buf("stage")
        for g in groups:
            np_ = sum(r for _, r, _, _ in g)
            for name, r, p0, c0 in g:
                src = self.dram[name].rearrange("(r p) -> r p", p=128)
                S.dma("sp", "stage", lambda e, src=src, p0=p0, r=r: e.dma_start(out=stage[p0:p0 + r, :], in_=src),
                      writes=[b_stage])
            c0 = g[0][3]
            S.op("pe", lambda e, np_=np_: e.transpose(out=self.bank(7)[:, 0:np_], in_=stage[0:np_, :],
                                                      identity=self.ident[0:np_, 0:np_]),
                 reads=[b_stage, self.b_ident], writes=[self.pb[7]])
            S.op("dve", lambda e, np_=np_, c0=c0: e.tensor_copy(out=self.vecs_sb[:, c0:c0 + np_],
                                                                in_=self.bank(7)[:, 0:np_]),
                 reads=[self.pb[7]], writes=[self.b_vecs])
        self.ar_base = ar.off

    def vec(self, name, c):
        col = self.vec_col[name] + c
        return self.vecs_sb[:, col:col + 1]

    def load_x(self):
        nc, S, ar = self.nc, self.S, self.ar
        ar.reset(self.ar_base)
        xin = [ar.f32(1024), ar.f32(1024)]
        b_in = [Buf("xin0"), Buf("xin1")]
        xv = self.dram["x"].rearrange("(t p) d -> t p d", p=128)
        for t in range(32):
            sl = t % 2
            S.dma("sp", "xin%d" % sl, lambda e, t=t, sl=sl: e.dma_start(out=xin[sl], in_=xv[t]),
                  writes=[b_in[sl]])
            for half in range(2):
                bk = 2 * sl + half
                for cc in range(4):
                    c = half * 4 + cc
                    S.op("pe", lambda e, c=c, cc=cc, sl=sl, bk=bk: e.transpose(
                        out=self.bank(bk)[:, cc * 128:(cc + 1) * 128],
                        in_=xin[sl][:, c * 128:(c + 1) * 128], identity=self.ident),
                        reads=[b_in[sl], self.b_ident], writes=[self.pb[bk]])
                eng = "dve" if half == 0 else "act"
                def cp(e, half=half, bk=bk, t=t, eng=eng):
                    o = self.X[:, half * 4:(half + 1) * 4, t * 128:(t + 1) * 128]
                    i = self.bank(bk).rearrange("p (c n) -> p c n", c=4)
                    if eng == "dve":
                        return e.tensor_copy(out=o, in_=i)
                    return e.activation(out=o, in_=i, func=AF.Copy)
                S.op(eng, cp, reads=[self.pb[bk]],
                     writes=[self.xb[c][t // 4] for c in range(half * 4, half * 4 + 4)])

    def store_y(self):
        nc, S, ar = self.nc, self.S, self.ar
        yo = [ar.f32(1024), ar.f32(1024)]
        b_yo = [Buf("yo0"), Buf("yo1")]
        junk = ar.f32(1024)
        b_junk = Buf("junk")
        stat = [ar.f32(4), ar.f32(4)]
        b_stat = [Buf("st0"), Buf("st1")]
        yv = self.dram["y"].rearrange("(t p) d -> t p d", p=128)
        if self.final_norm:
            gbc = ar.f32(1024)
            b_g = Buf("gbc")
            S.dma("sp", "gbc", lambda e: e.dma_start(out=gbc, in_=self.dram["final_norm"].partition_broadcast(128)),
                  writes=[b_g])
        toks = []
        for t in range(32):
            sl = t % 2
            bk = [4 * sl, 4 * sl + 1]
            for half in range(2):
                for cc in range(4):
                    c = half * 4 + cc
                    S.op("pe", lambda e, c=c, cc=cc, t=t, b=bk[half]: e.transpose(
                        out=self.bank(b)[:, cc * 128:(cc + 1) * 128],
                        in_=self.X[:, c, t * 128:(t + 1) * 128], identity=self.ident),
                        reads=[self.xb[c][t // 4], self.b_ident], writes=[self.pb[bk[half]]])
            pin = self.ps[:, 4 * sl:4 * sl + 2, :]
            if self.final_norm:
                S.op("act", lambda e, pin=pin, sl=sl: e.activation(
                    out=junk.rearrange("p (a n) -> p a n", a=2), in_=pin, func=AF.Square,
                    accum_out=stat[sl][:, 0:1]),
                    reads=[self.pb[bk[0]], self.pb[bk[1]]], writes=[b_junk, b_stat[sl]])
                S.op("act", lambda e, sl=sl: e.activation(out=stat[sl][:, 1:2], in_=stat[sl][:, 0:1], func=AF.Ln,
                                                          scale=1.0 / D, bias=EPS),
                     reads=[b_stat[sl]], writes=[b_stat[sl]])
                S.op("act", lambda e, sl=sl: e.activation(out=stat[sl][:, 2:3], in_=stat[sl][:, 1:2], func=AF.Exp,
                                                          scale=-0.5),
                     reads=[b_stat[sl]], writes=[b_stat[sl]])
                S.op("dve", lambda e, pin=pin, sl=sl: e.scalar_tensor_tensor(
                    out=yo[sl].rearrange("p (a n) -> p a n", a=2), in0=pin, scalar=stat[sl][:, 2:3],
                    in1=gbc.rearrange("p (a n) -> p a n", a=2), op0=ALU.mult, op1=ALU.mult),
                    reads=[self.pb[bk[0]], self.pb[bk[1]], b_stat[sl], b_g], writes=[b_yo[sl]])
            else:
                S.op("dve", lambda e, pin=pin, sl=sl: e.tensor_copy(
                    out=yo[sl].rearrange("p (a n) -> p a n", a=2), in_=pin),
                    reads=[self.pb[bk[0]], self.pb[bk[1]]], writes=[b_yo[sl]])
            tok = S.dma("sp", "yout%d" % sl, lambda e, t=t, sl=sl: e.dma_start(out=yv[t], in_=yo[sl]),
                        reads=[b_yo[sl]])
            toks.append(tok)
        S.final_wait("sp", toks[-2:])

    def emit_norm(self, gname, t0, ntile, xn, b_xn, tmp):
        S = self.S
        sq, b_sq, lnv, b_ln = tmp
        for j in range(ntile):
            t = t0 + j
            tok = slice(t * 512, (t + 1) * 512)
            for hf in range(2):
                S.op("act", lambda e, hf=hf, tok=tok: e.activation(
                    out=sq[hf], in_=self.X[:, hf * 4:(hf + 1) * 4, tok], func=AF.Square),
                    reads=[self.xb[c][t] for c in range(hf * 4, hf * 4 + 4)], writes=[b_sq[hf]])
                for cc in range(4):
                    c = hf * 4 + cc
                    S.op("pe", lambda e, hf=hf, cc=cc, c=c: e.matmul(
                        self.bank(7), lhsT=self.ones_bf, rhs=sq[hf][:, cc, :], start=(c == 0), stop=(c == 7)),
                        reads=[b_sq[hf], self.b_ones], writes=[self.pb[7]])
            S.op("act", lambda e: e.activation(out=lnv[0], in_=self.bank(7), func=AF.Ln, scale=1.0 / D,
                                               bias=EPS),
                 reads=[self.pb[7]], writes=[b_ln[0]])
            S.op("act", lambda e: e.activation(out=lnv[1], in_=lnv[0], func=AF.Exp, scale=-0.5),
                 reads=[b_ln[0]], writes=[b_ln[1]])
            for c in range(NCH):
                S.op("dve", lambda e, c=c, tok=tok, j=j: e.scalar_tensor_tensor(
                    out=xn[:, c, j * 512:(j + 1) * 512], in0=self.X[:, c, tok], scalar=self.vec(gname, c),
                    in1=lnv[1], op0=ALU.mult, op1=ALU.mult),
                    reads=[self.xb[c][t], b_ln[1], self.b_vecs], writes=[b_xn[j]])

    def norm_tmp(self, small=False):
        ar = self.ar
        if small:
            a = ar.bf16(4 * 512).rearrange("p (a n) -> p a n", a=4)
            l = ar.f32(512)
            b, bl = Buf("sq"), Buf("ln")
            return ([a, a], [b, b], [l, l], [bl, bl])
        sq = [ar.bf16(4 * 512).rearrange("p (a n) -> p a n", a=4) for _ in range(2)]
        lnv = [ar.f32(512), ar.f32(512)]
        return (sq, [Buf("sq0"), Buf("sq1")], lnv, [Buf("ln0"), Buf("ln1")])

    def emit_ffn(self, pfx):
        S, ar = self.S, self.ar
        w_in = self.dram[pfx + "_w_in"].rearrange("(k p) n -> p k n", p=128)
        w_out = self.dram[pfx + "_w_out"].rearrange("(f p) n -> p f n", p=128)
        TB = 4
        xn = ar.bf16(NCH * TB * 512).rearrange("p (c n) -> p c n", c=NCH)
        b_xn = [Buf("xn%d" % j) for j in range(TB)]
        ntmp = self.norm_tmp()
        win = [ar.bf16(8 * 512).rearrange("p (k n) -> p k n", k=8) for _ in range(2)]
        wout = [ar.bf16(2 * 1024).rearrange("p (f n) -> p f n", f=2) for _ in range(2)]
        b_w = [Buf("w0"), Buf("w1")]
        sg = [ar.f32(512), ar.f32(512)]
        b_sg = [Buf("sg0"), Buf("sg1")]
        gT = [ar.bf16(2 * 512).rearrange("p (f n) -> p f n", f=2) for _ in range(2)]
        b_g = [[Buf("g%d%d" % (i, f)) for f in range(2)] for i in range(2)]
        NFB = DFF // 256
        ybank = [4, 5, 6]
        ycnt = 0
        seq = [(tb, fb) for tb in range(SEQ // (TB * 512)) for fb in range(NFB)]

        def load_w(idx):
            tb, fb = seq[idx]
            sl = idx % 2
            S.dma("pool", "w%d" % sl, lambda e: e.dma_start(out=win[sl][:, :, 0:256],
                                                            in_=w_in[:, :, fb * 256:(fb + 1) * 256]),
                  writes=[b_w[sl]])
            S.dma("pool", "w%d" % sl, lambda e: e.dma_start(out=win[sl][:, :, 256:512],
                                                            in_=w_in[:, :, DFF + fb * 256:DFF + (fb + 1) * 256]),
                  writes=[b_w[sl]])
            S.dma("pool", "w%d" % sl, lambda e: e.dma_start(out=wout[sl], in_=w_out[:, 2 * fb:2 * fb + 2, :]),
                  writes=[b_w[sl]])

        load_w(0)
        for idx, (tb, fb) in enumerate(seq):
            sl = idx % 2
            if idx + 1 < len(seq):
                load_w(idx + 1)
            if fb == 0:
                self.emit_norm(pfx + "_norm", tb * TB, TB, xn, b_xn, ntmp)

            def p1(j, sl=sl):
                par = j % 2
                for fc in range(2):
                    for gu in range(2):
                        bk = fc * 2 + gu
                        for k in range(8):
                            S.op("pe", lambda e, bk=bk, k=k, fc=fc, gu=gu, j=j: e.matmul(
                                self.bank(bk), lhsT=win[sl][:, k, gu * 256 + fc * 128:gu * 256 + (fc + 1) * 128],
                                rhs=xn[:, k, j * 512:(j + 1) * 512], start=(k == 0), stop=(k == 7)),
                                reads=[b_w[sl], b_xn[j]], writes=[self.pb[bk]])
                    S.op("act", lambda e, fc=fc: e.activation(out=sg[fc], in_=self.bank(fc * 2), func=AF.Silu),
                         reads=[self.pb[fc * 2]], writes=[b_sg[fc]])
                    S.op("dve", lambda e, fc=fc, par=par: e.tensor_tensor(
                        out=gT[par][:, fc, :], in0=self.bank(fc * 2 + 1), in1=sg[fc], op=ALU.mult),
                        reads=[self.pb[fc * 2 + 1], b_sg[fc]], writes=[b_g[par][fc]])

            def p2(j, sl=sl, tb=tb):
                nonlocal ycnt
                par = j % 2
                t = tb * TB + j
                tok = slice(t * 512, (t + 1) * 512)
                for dc in range(NCH):
                    bk = ybank[ycnt % 3]
                    ycnt += 1
                    for fc in range(2):
                        S.op("pe", lambda e, bk=bk, fc=fc, dc=dc, par=par: e.matmul(
                            self.bank(bk), lhsT=wout[sl][:, fc, dc * 128:(dc + 1) * 128], rhs=gT[par][:, fc, :],
                            start=(fc == 0), stop=(fc == 1)),
                            reads=[b_w[sl], b_g[par][fc]], writes=[self.pb[bk]])
                    S.op("dve", lambda e, bk=bk, dc=dc, tok=tok: e.scalar_tensor_tensor(
                        out=self.X[:, dc, tok], in0=self.bank(bk), scalar=0.5, in1=self.X[:, dc, tok],
                        op0=ALU.mult, op1=ALU.add),
                        reads=[self.pb[bk], self.xb[dc][t]], writes=[self.xb[dc][t]])

            p1(0)
            for j in range(TB):
                if j + 1 < TB:
                    p1(j + 1)
                p2(j)


    def emit_gla(self, lp):
        S, ar = self.S, self.ar
        p = lp + "_gla"
        w_in = self.dram[p + "_w_in"].rearrange("(k p) n -> p k n", p=128)
        w_out = self.dram[p + "_w_out"].rearrange("(f p) n -> p f n", p=128)
        w_gk = self.dram[p + "_w_gk_up"]
        xn = ar.bf16(NCH * 512).rearrange("p (c n) -> p c n", c=NCH)
        b_xn = [Buf("xn")]
        ntmp = self.norm_tmp()
        wqk = [ar.bf16(8 * 256).rearrange("p (k n) -> p k n", k=8) for _ in range(2)]
        wvr = [ar.bf16(8 * 512).rearrange("p (k n) -> p k n", k=8) for _ in range(2)]
        wo = [ar.bf16(2 * 1024).rearrange("p (f n) -> p f n", f=2) for _ in range(2)]
        wgk = [ar.bf16(128) for _ in range(2)]
        b_w = [Buf("gw0"), Buf("gw1")]
        wg = ar.bf16(8 * 16).rearrange("p (k n) -> p k n", k=8)
        b_wg = Buf("wg")
        nb = ar.f32(4)
        b_nb = Buf("nb")
        gain = ar.f32(256)
        b_gain = Buf("gain")
        cmask = ar.f32(128)
        b_cm = Buf("cmask")
        state = ar.f32(4 * 256).rearrange("p (h n) -> p h n", h=4)
        b_st = [Buf("st%d" % h) for h in range(4)]
        sbf = ar.bf16(256)
        b_sbf = Buf("sbf")
        gkl = ar.bf16(512)
        b_gkl = Buf("gkl")
        sp = ar.f32(512)
        b_sp = Buf("sp")
        cs = ar.f32(512)
        b_cs = Buf("cs")
        eb = ar.f32(512)
        b_eb = Buf("eb")
        enb = ar.f32(512)
        b_enb = Buf("enb")
        qd = ar.bf16(512)
        b_qd = Buf("qd")
        ki = ar.bf16(512)
        b_ki = Buf("ki")
        am = ar.bf16(128)
        b_am = Buf("am")
        kiT = ar.bf16(128)
        b_kiT = Buf("kiT")
        vsb = ar.bf16(256)
        b_v = Buf("vsb")
        sr2 = [ar.bf16(256), ar.bf16(256)]
        b_sr2 = [Buf("sr0"), Buf("sr1")]
        s1 = ar.f32(256)
        b_s1 = Buf("s1")
        on = ar.f32(256)
        b_on = Buf("on")
        og = ar.bf16(256)
        b_og = Buf("og")
        ogT = ar.bf16(2 * 512).rearrange("p (f n) -> p f n", f=2)
        b_ogT = Buf("ogT")
        stt = ar.f32(4)
        b_stt = Buf("stt")
        pb = self.pb
        pb3b, pb5b = pb[0], pb[1]
        bank3bf = self.bank(0).bitcast(BF16)
        bank5bf = self.bank(1).bitcast(BF16)

        S.dma("pool", "gconst", lambda e: e.dma_start(out=wg, in_=w_in[:, :, 3072:3088]), writes=[b_wg])
        S.dma("sp", "gconst2", lambda e: e.dma_start(out=gain, in_=self.dram[p + "_o_norm"].partition_broadcast(128)),
              writes=[b_gain])
        c0 = self.vec_col[p + "_b_gk"]
        S.op("dve", lambda e: e.tensor_scalar(out=nb, in0=self.vecs_sb[:, c0:c0 + 4], scalar1=-1.0, scalar2=None,
                                              op0=ALU.mult), reads=[self.b_vecs], writes=[b_nb])
        S.op("pool", lambda e: e.affine_select(out=cmask, in_=self.ones_f32, pattern=[[1, 128]],
                                               compare_op=ALU.is_ge, fill=0.0, base=0, channel_multiplier=-1),
             reads=[self.b_ones32], writes=[b_cm])
        S.op("dve", lambda e: e.memset(state, 0.0), writes=b_st)

        seq = [(tb, h) for tb in range(SEQ // 512) for h in range(4)]
        dbg = int(os.environ.get("K_DBG", "99"))
        if dbg < 99:
            seq = seq[:1]

        def load_w(idx):
            tb, h = seq[idx]
            sl = idx % 2
            nm = "gw%d" % sl
            S.dma("pool", nm, lambda e: e.dma_start(out=wqk[sl][:, :, 0:128], in_=w_in[:, :, h * 128:(h + 1) * 128]),
                  writes=[b_w[sl]])
            S.dma("pool", nm, lambda e: e.dma_start(out=wqk[sl][:, :, 128:256],
                                                    in_=w_in[:, :, 512 + h * 128:512 + (h + 1) * 128]),
                  writes=[b_w[sl]])
            S.dma("pool", nm, lambda e: e.dma_start(out=wvr[sl][:, :, 0:256],
                                                    in_=w_in[:, :, 1024 + h * 256:1024 + (h + 1) * 256]),
                  writes=[b_w[sl]])
            S.dma("pool", nm, lambda e: e.dma_start(out=wvr[sl][:, :, 256:512],
                                                    in_=w_in[:, :, 2048 + h * 256:2048 + (h + 1) * 256]),
                  writes=[b_w[sl]])
            S.dma("pool", nm, lambda e: e.dma_start(out=wo[sl], in_=w_out[:, 2 * h:2 * h + 2, :]), writes=[b_w[sl]])
            S.dma("pool", nm, lambda e: e.dma_start(out=wgk[sl][0:16, :], in_=w_gk[:, h * 128:(h + 1) * 128]),
                  writes=[b_w[sl]])

        def head(tb, h, sl):
            tok0 = tb * 512
            bw = b_w[sl]
            for qk in range(2):
                for k in range(8):
                    S.op("pe", lambda e, qk=qk, k=k: e.matmul(
                        self.bank(qk), lhsT=wqk[sl][:, k, qk * 128:(qk + 1) * 128], rhs=xn[:, k, :],
                        start=(k == 0), stop=(k == 7)), reads=[bw, b_xn[0]], writes=[pb[qk]])
            S.op("pe", lambda e: e.matmul(self.bank(2), lhsT=wgk[sl][0:16, :], rhs=gkl[0:16, :], start=True, stop=True),
                 reads=[bw, b_gkl], writes=[pb[2]])
            S.op("act", lambda e: e.activation(out=sp, in_=self.bank(2), func=AF.Exp, scale=-1.0, bias=nb[:, h:h + 1]),
                 reads=[pb[2], b_nb], writes=[b_sp])
            S.op("act", lambda e: e.activation(out=sp, in_=sp, func=AF.Ln, scale=1.0, bias=1.0),
                 reads=[b_sp], writes=[b_sp])
            for c in range(4):
                S.op("dve", lambda e, c=c: e.tensor_tensor_scan(
                    out=cs[:, c * 128:(c + 1) * 128], data0=self.ones_f32, data1=sp[:, c * 128:(c + 1) * 128],
                    initial=0.0, op0=ALU.mult, op1=ALU.add), reads=[b_sp, self.b_ones32], writes=[b_cs])
            S.op("act", lambda e: e.activation(out=eb, in_=cs, func=AF.Exp, scale=-1.0 / 16), reads=[b_cs], writes=[b_eb])
            S.op("act", lambda e: e.activation(out=enb, in_=cs, func=AF.Exp, scale=1.0 / 16), reads=[b_cs], writes=[b_enb])
            S.op("dve", lambda e: e.scalar_tensor_tensor(out=qd, in0=self.bank(0), scalar=128.0 ** -0.5, in1=eb,
                                                         op0=ALU.mult, op1=ALU.mult),
                 reads=[pb[0], b_eb], writes=[b_qd])
            S.op("dve", lambda e: e.tensor_tensor(out=ki, in0=self.bank(1), in1=enb, op=ALU.mult),
                 reads=[pb[1], b_enb], writes=[b_ki])
            S.op("act", lambda e: e.activation(out=sbf, in_=state[:, h, :], func=AF.Copy),
                 reads=[b_st[h]], writes=[b_sbf])
            def front(c):
                cs_ = slice(c * 128, (c + 1) * 128)
                sp_ = c % 2
                S.op("pe", lambda e: e.matmul(self.bank(3)[:, 0:128], lhsT=ki[:, cs_], rhs=qd[:, cs_], start=True, stop=True),
                     reads=[b_ki, b_qd], writes=[pb[3]])
                S.op("pe", lambda e: e.transpose(out=bank3bf[:, 0:128], in_=ki[:, cs_], identity=self.ident_bf),
                     reads=[b_ki, self.b_ident], writes=[pb3b])
                for k in range(8):
                    S.op("pe", lambda e, k=k: e.matmul(self.bank(4), lhsT=xn[:, k, cs_], rhs=wvr[sl][:, k, :],
                                                       start=(k == 0), stop=(k == 7)),
                         reads=[bw, b_xn[0]], writes=[pb[4]])
                S.op("dve", lambda e: e.tensor_tensor(out=am, in0=self.bank(3)[:, 0:128], in1=cmask, op=ALU.mult),
                     reads=[pb[3], b_cm], writes=[b_am])
                S.op("act", lambda e: e.activation(out=kiT, in_=bank3bf[:, 0:128], func=AF.Copy),
                     reads=[pb3b], writes=[b_kiT])
                S.op("act", lambda e: e.activation(out=vsb, in_=self.bank(4)[:, 0:256], func=AF.Copy),
                     reads=[pb[4]], writes=[b_v])
                S.op("act", lambda e: e.activation(out=sr2[sp_], in_=self.bank(4)[:, 256:512], func=AF.Silu),
                     reads=[pb[4]], writes=[b_sr2[sp_]])

            def back1(c):
                cs_ = slice(c * 128, (c + 1) * 128)
                last = slice(c * 128 + 127, c * 128 + 128)
                sp_ = c % 2
                S.op("pe", lambda e: e.matmul(self.bank(5)[:, 0:256], lhsT=am, rhs=vsb, start=True, stop=False),
                     reads=[b_am, b_v], writes=[pb[5]])
                S.op("pe", lambda e: e.matmul(self.bank(5)[:, 0:256], lhsT=qd[:, cs_], rhs=sbf, start=False, stop=True),
                     reads=[b_qd, b_sbf], writes=[pb[5]])
                S.op("pe", lambda e: e.matmul(self.bank(6)[:, 0:256], lhsT=kiT, rhs=vsb, start=True, stop=True),
                     reads=[b_kiT, b_v], writes=[pb[6]])
                S.op("dve", lambda e: e.tensor_scalar(out=s1, in0=state[:, h, :], scalar1=eb[:, last], scalar2=None,
                                                      op0=ALU.mult), reads=[b_st[h], b_eb], writes=[b_s1])
                S.op("dve", lambda e: e.scalar_tensor_tensor(out=state[:, h, :], in0=self.bank(6)[:, 0:256],
                                                             scalar=eb[:, last], in1=s1, op0=ALU.mult, op1=ALU.add),
                     reads=[pb[6], b_eb, b_s1], writes=[b_st[h]])
                S.op("act", lambda e: e.activation(out=sbf, in_=state[:, h, :], func=AF.Copy),
                     reads=[b_st[h]], writes=[b_sbf])
                S.op("act", lambda e: e.activation(out=on, in_=self.bank(5)[:, 0:256], func=AF.Square,
                                                   accum_out=stt[:, 0:1]), reads=[pb[5]], writes=[b_on, b_stt])
                S.op("act", lambda e: e.activation(out=stt[:, 1:2], in_=stt[:, 0:1], func=AF.Ln, scale=1.0 / 256, bias=EPS),
                     reads=[b_stt], writes=[b_stt])
                S.op("act", lambda e: e.activation(out=stt[:, 2:3], in_=stt[:, 1:2], func=AF.Exp, scale=-0.5),
                     reads=[b_stt], writes=[b_stt])
                S.op("dve", lambda e: e.scalar_tensor_tensor(out=on, in0=self.bank(5)[:, 0:256], scalar=stt[:, 2:3],
                                                             in1=gain, op0=ALU.mult, op1=ALU.mult),
                     reads=[pb[5], b_stt, b_gain], writes=[b_on])
                S.op("dve", lambda e: e.tensor_tensor(out=og, in0=on, in1=sr2[sp_], op=ALU.mult),
                     reads=[b_on, b_sr2[sp_]], writes=[b_og])

            def back2(c):
                cs_ = slice(c * 128, (c + 1) * 128)
                for dvc in range(2):
                    S.op("pe", lambda e, dvc=dvc: e.transpose(
                        out=bank5bf[:, dvc * 128:(dvc + 1) * 128], in_=og[:, dvc * 128:(dvc + 1) * 128],
                        identity=self.ident_bf), reads=[b_og, self.b_ident], writes=[pb5b])
                S.op("act", lambda e: e.activation(
                    out=ogT[:, :, cs_], in_=bank5bf[:, 0:256].rearrange("p (f n) -> p f n", f=2), func=AF.Copy),
                    reads=[pb5b], writes=[b_ogT])

            front(0)
            for c in range(4):
                back1(c)
                if c + 1 < 4:
                    front(c + 1)
                back2(c)
            tok = slice(tok0, tok0 + 512)
            for dc in range(NCH):
                for dvc in range(2):
                    S.op("pe", lambda e, dc=dc, dvc=dvc: e.matmul(
                        self.bank(7), lhsT=wo[sl][:, dvc, dc * 128:(dc + 1) * 128], rhs=ogT[:, dvc, :],
                        start=(dvc == 0), stop=(dvc == 1)), reads=[bw, b_ogT], writes=[pb[7]])
                S.op("dve", lambda e, dc=dc: e.tensor_tensor(out=self.X[:, dc, tok], in0=self.bank(7),
                                                             in1=self.X[:, dc, tok], op=ALU.add),
                     reads=[pb[7], self.xb[dc][tb]], writes=[self.xb[dc][tb]])

        load_w(0)
        for idx, (tb, h) in enumerate(seq):
            if idx + 1 < len(seq):
                load_w(idx + 1)
            if h == 0:
                self.emit_norm(lp + "_mix_norm", tb, 1, xn, b_xn, ntmp)
                for k in range(8):
                    S.op("pe", lambda e, k=k: e.matmul(self.bank(2)[0:16, :], lhsT=wg[:, k, :], rhs=xn[:, k, :],
                                                       start=(k == 0), stop=(k == 7)),
                         reads=[b_wg, b_xn[0]], writes=[pb[2]])
                S.op("act", lambda e: e.activation(out=gkl[0:16, :], in_=self.bank(2)[0:16, :], func=AF.Copy),
                     reads=[pb[2]], writes=[b_gkl])
            if dbg >= 1:
                head(tb, h, idx % 2)


    def emit_swa(self, lp):
        S, ar = self.S, self.ar
        p = lp + "_swa"
        pb = self.pb
        wqkv_d = self.dram[p + "_w_qkv"].rearrange("(k p) n -> p k n", p=128)
        wout_d = self.dram[p + "_w_out"].rearrange("(k p) n -> p k n", p=128)
        wqkv = ar.bf16(8 * 1280).rearrange("p (k n) -> p k n", k=8)
        wout = ar.bf16(8 * 1024).rearrange("p (k n) -> p k n", k=8)
        brow = ar.bf16(1280)
        b_wq, b_wo, b_brow = Buf("wqkv"), Buf("wout"), Buf("brow")
        for k in range(8):
            S.dma("pool", "swq", lambda e, k=k: e.dma_start(out=wqkv[:, k, :], in_=wqkv_d[:, k, :]), writes=[b_wq])
        S.op("dve", lambda e: e.memset(brow, 0.0), writes=[b_brow])
        S.dma("pool", "sbr", lambda e: e.dma_start(out=brow[0:1, :],
                                                   in_=self.dram[p + "_b_qkv"].rearrange("(a n) -> a n", a=1)),
              writes=[b_brow])
        for k in range(8):
            S.dma("pool", "swo", lambda e, k=k: e.dma_start(out=wout[:, k, :], in_=wout_d[:, k, :]), writes=[b_wo])
        sinks = ar.f32(16)
        b_sk = Buf("sinks")
        S.dma("sp", "ssk", lambda e: e.dma_start(out=sinks, in_=self.dram[p + "_sinks"].partition_broadcast(128)),
              writes=[b_sk])
        amask = ar.f32(256)
        b_am = Buf("amask")
        cos = ar.f32(256).rearrange("p (t i) -> p t i", t=32)
        sin = ar.f32(256).rearrange("p (t i) -> p t i", t=32)
        b_cs = Buf("cossin")
        mark = ar.off
        zer = ar.f32(128)
        b_z = Buf("zer")
        S.op("pool", lambda e: e.memset(zer, 0.0), writes=[b_z])
        S.op("pool", lambda e: e.affine_select(out=amask[:, 0:128], in_=zer, pattern=[[1, 128]], compare_op=ALU.is_ge,
                                               fill=-30000.0, base=-1, channel_multiplier=-1),
             reads=[b_z], writes=[b_am])
        S.op("pool", lambda e: e.affine_select(out=amask[:, 128:256], in_=zer, pattern=[[-1, 128]], compare_op=ALU.is_ge,
                                               fill=-30000.0, base=0, channel_multiplier=1),
             reads=[b_z], writes=[b_am])
        posi = ar.i32(128)
        posf = ar.f32(128)
        pT_ = ar.f32(32)
        ang = ar.f32(256)
        u = ar.f32(256)
        ni = ar.i32(256)
        nf = ar.f32(256)
        m1 = ar.f32(256)
        bt = Buf("ropetmp")
        S.dma("sp", "spos", lambda e: e.dma_start(out=posi[0:32, :], in_=self.dram["pos"].rearrange("(t p) -> t p", p=128)),
              writes=[bt])
        S.op("dve", lambda e: e.tensor_copy(out=posf[0:32, :], in_=posi[0:32, :]), reads=[bt], writes=[bt])
        S.op("pe", lambda e: e.transpose(out=self.bank(0)[:, 0:32], in_=posf[0:32, :], identity=self.ident[0:32, 0:32]),
             reads=[bt, self.b_ident], writes=[pb[0]])
        S.op("dve", lambda e: e.tensor_copy(out=pT_, in_=self.bank(0)[:, 0:32]), reads=[pb[0]], writes=[bt])
        ang3 = ang.rearrange("p (t i) -> p t i", t=32)
        for i in range(8):
            invf = float(500000.0 ** (-(2.0 * i) / 16.0))
            S.op("dve", lambda e, i=i, invf=invf: e.tensor_scalar(out=ang3[:, :, i], in0=pT_, scalar1=invf, scalar2=None,
                                                                  op0=ALU.mult), reads=[bt], writes=[bt])
        TWO_PI = 2.0 * math.pi
        for tab, shift in ((sin, 0.0), (cos, 0.25)):
            tabf = tab.rearrange("p t i -> p (t i)")
            S.op("dve", lambda e, shift=shift: e.tensor_scalar(out=u, in0=ang, scalar1=1.0 / TWO_PI, scalar2=shift,
                                                               op0=ALU.mult, op1=ALU.add), reads=[bt], writes=[bt])
            S.op("dve", lambda e: e.tensor_copy(out=ni, in_=u), reads=[bt], writes=[bt])
            S.op("dve", lambda e: e.tensor_copy(out=nf, in_=ni), reads=[bt], writes=[bt])
            S.op("dve", lambda e: e.tensor_tensor(out=u, in0=u, in1=nf, op=ALU.subtract), reads=[bt], writes=[bt])
            S.op("dve", lambda e: e.tensor_scalar(out=m1, in0=u, scalar1=0.5, scalar2=None, op0=ALU.is_gt),
                 reads=[bt], writes=[bt])
            S.op("dve", lambda e: e.tensor_tensor(out=u, in0=u, in1=m1, op=ALU.subtract), reads=[bt], writes=[bt])
            S.op("dve", lambda e: e.tensor_scalar(out=m1, in0=u, scalar1=-0.5, scalar2=None, op0=ALU.is_lt),
                 reads=[bt], writes=[bt])
            S.op("dve", lambda e: e.tensor_tensor(out=u, in0=u, in1=m1, op=ALU.add), reads=[bt], writes=[bt])
            S.op("act", lambda e, tabf=tabf: e.activation(out=tabf, in_=u, func=AF.Sin, scale=TWO_PI * (1.0 - 1e-6)),
                 reads=[bt], writes=[b_cs, bt])
        dbg = int(os.environ.get("K_DBG", "99"))
        S.barrier()
        ar.reset(mark)
        if dbg == 300:
            S.wait_bufs("pe", [b_wq, b_wo, b_brow, b_sk])
            return
        xn = ar.bf16(NCH * 512).rearrange("p (c n) -> p c n", c=NCH)
        b_xn = [Buf("xn")]
        ntmp = self.norm_tmp(small=True)
        qkvf = ar.f32(1152)
        b_qf = Buf("qkvf")
        rt = [ar.f32(144).rearrange("p (h i) -> p h i", h=18) for _ in range(3)]
        b_rt = Buf("rt")
        ksw = ar.f32(128)
        b_ksw = Buf("ksw")
        qT = ar.bf16(8 * 128).rearrange("p (h n) -> p h n", h=8)
        b_qT = Buf("qT")
        kT = [ar.bf16(2 * 128).rearrange("p (v n) -> p v n", v=2) for _ in range(2)]
        b_kT = [Buf("kT0"), Buf("kT1")]
        vsb = [ar.bf16(128) for _ in range(2)]
        b_v = [Buf("v0"), Buf("v1")]
        sc = ar.f32(512).rearrange("p (h s) -> p h s", h=2)
        b_sc = Buf("sc")
        _prb = ar.bf16(512).rearrange("p (h s) -> p h s", h=2)
        prb = [_prb, _prb]
        _bprb = Buf("prb")
        b_prb = [_bprb, _bprb]
        _pT = ar.bf16(512).rearrange("p (h k n) -> p h k n", h=2, k=2)
        pT = [_pT, _pT]
        _bpT = Buf("pT")
        b_pT = [_bpT, _bpT]
        sm = [ar.f32(12) for _ in range(2)]
        b_sm = [Buf("sm0"), Buf("sm1")]
        rinv = ar.f32(16)
        b_rinv = Buf("rinv")
        ao = ar.f32(1024)
        b_ao = Buf("ao")
        aoT = ar.bf16(8 * 128).rearrange("p (k n) -> p k n", k=8)
        b_aoT = Buf("aoT")
        bo_col = self.vec_col[p + "_b_out"]
        qv3 = qkvf.rearrange("p (h d) -> p h d", h=18)
        x1, x2 = qv3[:, :, 0:8], qv3[:, :, 8:16]
        ao_ps = self.ps[:, 4:6, :]

        for t in range(8):
            self.emit_norm(lp + "_mix_norm", t, 1, xn, b_xn, ntmp)
            for sub in range(4):
                n = t * 4 + sub
                if dbg < 99 and n >= (2 if dbg == 307 else 1):
                    S.wait_bufs("pe", [b_wq, b_wo, b_brow, b_sk])
                    return
                tk = slice(sub * 128, (sub + 1) * 128)
                tabs = slice(n * 128, (n + 1) * 128)
                cur, prv = n % 2, (n + 1) % 2
                for bi, (c0, c1) in enumerate(((0, 512), (512, 1024), (1024, 1280))):
                    o = self.bank(bi)[:, 0:c1 - c0]
                    for k in range(8):
                        S.op("pe", lambda e, o=o, k=k, c0=c0, c1=c1, tk=tk: e.matmul(
                            o, lhsT=xn[:, k, tk], rhs=wqkv[:, k, c0:c1], start=(k == 0), stop=False),
                            reads=[b_xn[0], b_wq], writes=[pb[bi]])
                    S.op("pe", lambda e, o=o, c0=c0, c1=c1: e.matmul(
                        o, lhsT=self.ones_bf, rhs=brow[:, c0:c1], start=False, stop=True),
                        reads=[self.b_ones, b_brow], writes=[pb[bi]])
                S.op("dve", lambda e: e.tensor_copy(out=qkvf[:, 0:1024].rearrange("p (b n) -> p b n", b=2),
                                                    in_=self.ps[:, 0:2, :]),
                     reads=[pb[0], pb[1]], writes=[b_qf])
                S.op("dve", lambda e: e.tensor_copy(out=qkvf[:, 1024:1152], in_=self.bank(2)[:, 0:128]),
                     reads=[pb[2]], writes=[b_qf])
                S.op("act", lambda e, cur=cur: e.activation(out=vsb[cur], in_=self.bank(2)[:, 128:256], func=AF.Copy),
                     reads=[pb[2]], writes=[b_v[cur]])
                cb = cos[:, n, :].unsqueeze(1).broadcast_to([128, 18, 8])
                sb_ = sin[:, n, :].unsqueeze(1).broadcast_to([128, 18, 8])
                rd = [b_qf, b_cs, b_rt]
                S.op("dve", lambda e, cb=cb: e.tensor_tensor(out=rt[0], in0=x1, in1=cb, op=ALU.mult), reads=rd, writes=[b_rt])
                S.op("dve", lambda e, sb_=sb_: e.tensor_tensor(out=rt[1], in0=x2, in1=sb_, op=ALU.mult), reads=rd, writes=[b_rt])
                S.op("dve", lambda e, sb_=sb_: e.tensor_tensor(out=rt[2], in0=x1, in1=sb_, op=ALU.mult), reads=rd, writes=[b_rt])
                S.op("dve", lambda e: e.tensor_tensor(out=x1, in0=rt[0], in1=rt[1], op=ALU.subtract),
                     reads=[b_rt], writes=[b_qf])
                S.op("dve", lambda e, cb=cb: e.tensor_tensor(out=rt[0], in0=x2, in1=cb, op=ALU.mult), reads=rd, writes=[b_rt])
                S.op("dve", lambda e: e.tensor_tensor(out=x2, in0=rt[0], in1=rt[2], op=ALU.add),
                     reads=[b_rt], writes=[b_qf])
                S.op("act", lambda e: e.activation(out=ksw[:, 0:64], in_=qkvf[:, 1088:1152], func=AF.Copy),
                     reads=[b_qf], writes=[b_ksw])
                S.op("act", lambda e: e.activation(out=ksw[:, 64:128], in_=qkvf[:, 1024:1088], func=AF.Copy),
                     reads=[b_qf], writes=[b_ksw])
                if dbg == 301:
                    continue
                for hp in range(8):
                    bk = 3 + hp // 4
                    S.op("pe", lambda e, hp=hp, bk=bk: e.transpose(
                        out=self.bank(bk)[:, (hp % 4) * 128:(hp % 4 + 1) * 128], in_=qkvf[:, hp * 128:(hp + 1) * 128],
                        identity=self.ident), reads=[b_qf, self.b_ident], writes=[pb[bk]])
                S.op("pe", lambda e: e.transpose(out=self.bank(5)[:, 0:128], in_=qkvf[:, 1024:1152], identity=self.ident),
                     reads=[b_qf, self.b_ident], writes=[pb[5]])
                S.op("pe", lambda e: e.transpose(out=self.bank(5)[:, 128:256], in_=ksw, identity=self.ident),
                     reads=[b_ksw, self.b_ident], writes=[pb[5]])
                S.op("dve", lambda e: e.tensor_copy(out=qT[:, 0:4, :], in_=self.bank(3).rearrange("p (h n) -> p h n", h=4)),
                     reads=[pb[3]], writes=[b_qT])
                S.op("act", lambda e: e.activation(out=qT[:, 4:8, :], in_=self.bank(4).rearrange("p (h n) -> p h n", h=4),
                                                   func=AF.Copy), reads=[pb[4]], writes=[b_qT])
                S.op("dve", lambda e, cur=cur: e.tensor_copy(
                    out=kT[cur], in_=self.bank(5)[:, 0:256].rearrange("p (v n) -> p v n", v=2)),
                    reads=[pb[5]], writes=[b_kT[cur]])
                if dbg == 302:
                    continue
                kbs = [0, 1] if n > 0 else [1]
                c0 = 0 if n > 0 else 128
                for pr in range(8):
                    if dbg == 303 and pr >= 1:
                        break
                    par = pr % 2
                    g = pr // 4
                    scb = 6 if par == 0 else 2
                    ptb = 1 if par == 0 else 0
                    ptbf = self.bank(ptb).bitcast(BF16)
                    for hh in range(2):
                        base = hh * 64
                        var = 0 if (g == 0) == (hh == 0) else 1
                        for kb in kbs:
                            slot = prv if kb == 0 else cur
                            S.op("pe", lambda e, scb=scb, hh=hh, kb=kb, base=base, pr=pr, var=var, slot=slot: e.matmul(
                                self.bank(scb + hh)[:, kb * 128:(kb + 1) * 128],
                                lhsT=qT[base:base + 64, pr, :], rhs=kT[slot][base:base + 64, var, :],
                                start=True, stop=True), reads=[b_qT, b_kT[slot]], writes=[pb[scb + hh]])
                    scv = self.ps[:, scb:scb + 2, c0:256]
                    mk = amask[:, c0:256].unsqueeze(1).broadcast_to([128, 2, 256 - c0])
                    smp = sm[par]
                    bsm = b_sm[par]
                    S.op("dve", lambda e, scv=scv, mk=mk, c0=c0: e.scalar_tensor_tensor(
                        out=sc[:, :, c0:256], in0=scv, scalar=0.125, in1=mk, op0=ALU.mult, op1=ALU.add),
                        reads=[pb[scb], pb[scb + 1], b_am], writes=[b_sc])
                    S.op("dve", lambda e, smp=smp, c0=c0: e.tensor_reduce(out=smp[:, 0:2], in_=sc[:, :, c0:256], axis=AX.X,
                                                                          op=ALU.max), reads=[b_sc], writes=[bsm])
                    S.op("dve", lambda e, smp=smp, pr=pr: e.tensor_tensor(out=smp[:, 0:2], in0=smp[:, 0:2],
                                                                          in1=sinks[:, 2 * pr:2 * pr + 2], op=ALU.max),
                         reads=[bsm, b_sk], writes=[bsm])
                    S.op("dve", lambda e, smp=smp: e.tensor_scalar(out=smp[:, 2:4], in0=smp[:, 0:2], scalar1=-1.0,
                                                                   scalar2=None, op0=ALU.mult), reads=[bsm], writes=[bsm])
                    for hh in range(2):
                        S.op("act", lambda e, hh=hh, smp=smp, par=par, c0=c0: e.activation(
                            out=prb[par][:, hh, c0:256], in_=sc[:, hh, c0:256], func=AF.Exp, bias=smp[:, 2 + hh:3 + hh],
                            scale=1.0, accum_out=smp[:, 4 + hh:5 + hh]), reads=[b_sc, bsm], writes=[b_prb[par], bsm])
                    S.op("dve", lambda e, smp=smp, pr=pr: e.tensor_tensor(out=smp[:, 6:8], in0=sinks[:, 2 * pr:2 * pr + 2],
                                                                          in1=smp[:, 2:4], op=ALU.add),
                         reads=[bsm, b_sk], writes=[bsm])
                    S.op("act", lambda e, smp=smp: e.activation(out=smp[:, 8:10], in_=smp[:, 6:8], func=AF.Exp),
                         reads=[bsm], writes=[bsm])
                    S.op("dve", lambda e, smp=smp: e.tensor_tensor(out=smp[:, 10:12], in0=smp[:, 4:6], in1=smp[:, 8:10],
                                                                   op=ALU.add), reads=[bsm], writes=[bsm])
                    S.op("dve", lambda e, smp=smp, pr=pr: e.reciprocal(out=rinv[:, 2 * pr:2 * pr + 2], in_=smp[:, 10:12]),
                         reads=[bsm], writes=[b_rinv])
                    for hh in range(2):
                        for kb in kbs:
                            S.op("pe", lambda e, hh=hh, kb=kb, par=par, ptbf=ptbf: e.transpose(
                                out=ptbf[:, (hh * 2 + kb) * 128:(hh * 2 + kb + 1) * 128],
                                in_=prb[par][:, hh, kb * 128:(kb + 1) * 128], identity=self.ident_bf),
                                reads=[b_prb[par], self.b_ident], writes=[pb[ptb]])
                    kb0 = kbs[0]
                    for hh in range(2):
                        src = ptbf[:, hh * 256 + kb0 * 128:(hh + 1) * 256].rearrange("p (k n) -> p k n", k=2 - kb0)
                        dst = pT[par][:, hh, kb0:2, :]
                        if hh == 0:
                            S.op("act", lambda e, src=src, dst=dst: e.activation(out=dst, in_=src, func=AF.Copy),
                                 reads=[pb[ptb]], writes=[b_pT[par]])
                        else:
                            S.op("dve", lambda e, src=src, dst=dst: e.tensor_copy(out=dst, in_=src),
                                 reads=[pb[ptb]], writes=[b_pT[par]])
                    for hh in range(2):
                        h = 2 * pr + hh
                        abk = 4 + h // 8
                        oreg = self.bank(abk)[:, (h % 8) * 64:(h % 8 + 1) * 64]
                        for kb in kbs:
                            slot = prv if kb == 0 else cur
                            S.op("pe", lambda e, oreg=oreg, hh=hh, kb=kb, par=par, slot=slot, g=g, kbs=kbs: e.matmul(
                                oreg, lhsT=pT[par][:, hh, kb, :], rhs=vsb[slot][:, g * 64:(g + 1) * 64],
                                start=(kb == kbs[0]), stop=(kb == kbs[-1])),
                                reads=[b_pT[par], b_v[slot]], writes=[pb[abk]])
                if dbg in (303, 304):
                    continue
                S.op("dve", lambda e: e.tensor_tensor(
                    out=ao.rearrange("p (h d) -> p h d", h=16),
                    in0=ao_ps.rearrange("p b (h d) -> p (b h) d", h=8),
                    in1=rinv.unsqueeze(2).broadcast_to([128, 16, 64]), op=ALU.mult),
                    reads=[pb[4], pb[5], b_rinv], writes=[b_ao])
                if dbg == 3050:
                    continue
                for k in range(8):
                    tb_ = k // 4
                    S.op("pe", lambda e, k=k, tb_=tb_: e.transpose(
                        out=self.bank(tb_)[:, (k % 4) * 128:(k % 4 + 1) * 128],
                        in_=ao[:, k * 128:(k + 1) * 128], identity=self.ident),
                        reads=[b_ao, self.b_ident], writes=[pb[tb_]])
                S.op("act", lambda e: e.activation(out=aoT[:, 0:4, :], in_=self.bank(0).rearrange("p (k n) -> p k n", k=4),
                                                   func=AF.Copy), reads=[pb[0]], writes=[b_aoT])
                S.op("dve", lambda e: e.tensor_copy(out=aoT[:, 4:8, :], in_=self.bank(1).rearrange("p (k n) -> p k n", k=4)),
                     reads=[pb[1]], writes=[b_aoT])
                if dbg == 305:
                    continue
                for dc in range(NCH):
                    bk = 6 + dc // 4
                    oreg = self.bank(bk)[:, (dc % 4) * 128:(dc % 4 + 1) * 128]
                    for k in range(8):
                        S.op("pe", lambda e, oreg=oreg, k=k, dc=dc: e.matmul(
                            oreg, lhsT=wout[:, k, dc * 128:(dc + 1) * 128], rhs=aoT[:, k, :],
                            start=(k == 0), stop=(k == 7)), reads=[b_wo, b_aoT], writes=[pb[bk]])
                for dc in range(NCH):
                    bk = 6 + dc // 4
                    oreg = self.bank(bk)[:, (dc % 4) * 128:(dc % 4 + 1) * 128]
                    S.op("dve", lambda e, oreg=oreg, dc=dc, tabs=tabs: e.scalar_tensor_tensor(
                        out=self.X[:, dc, tabs], in0=oreg, scalar=self.vecs_sb[:, bo_col + dc:bo_col + dc + 1],
                        in1=self.X[:, dc, tabs], op0=ALU.add, op1=ALU.add),
                        reads=[pb[bk], self.b_vecs, self.xb[dc][t]], writes=[self.xb[dc][t]])


    def emit_sgu(self, lp):
        S, ar = self.S, self.ar
        p = lp + "_sgu"
        pb = self.pb
        w_in = self.dram[p + "_w_in"].rearrange("(k p) n -> p k n", p=128)
        w_out = self.dram[p + "_w_out"].rearrange("(f p) n -> p f n", p=128)
        w_s = self.dram[p + "_w_s"]
        b_s = self.dram[p + "_b_s"]
        gcol = self.vec_col[p + "_ln_gain"]
        bcol = self.vec_col[p + "_ln_bias"]
        TBT = 2
        NSUB = TBT * 4
        wct = ar.bf16(8 * 128).rearrange("p (g n) -> p g n", g=8)
        b_wct = Buf("wct")
        mark = ar.off
        stg = [ar.f32(128), ar.f32(128)]
        stg2 = [ar.f32(128), ar.f32(128)]
        b_stg = [Buf("stg0"), Buf("stg1")]
        b_stg2 = [Buf("stg20"), Buf("stg21")]
        for g in range(8):
            sl = g % 2
            S.dma("sp", "sws%d" % sl, lambda e, g=g, sl=sl: e.dma_start(out=stg[sl], in_=w_s[g]), writes=[b_stg[sl]])
            S.op("pool", lambda e, sl=sl: e.affine_select(out=stg2[sl], in_=stg[sl], pattern=[[-1, 128]],
                                                          compare_op=ALU.is_ge, fill=0.0, base=0, channel_multiplier=1),
                 reads=[b_stg[sl]], writes=[b_stg2[sl]])
            S.op("pe", lambda e, sl=sl: e.transpose(out=self.bank(sl)[:, 0:128], in_=stg2[sl], identity=self.ident),
                 reads=[b_stg2[sl], self.b_ident], writes=[pb[sl]])
            S.op("dve", lambda e, g=g, sl=sl: e.tensor_copy(out=wct[:, g, :], in_=self.bank(sl)[:, 0:128]),
                 reads=[pb[sl]], writes=[b_wct])
        S.barrier()
        ar.reset(mark)
        xn = ar.bf16(NCH * TBT * 512).rearrange("p (c n) -> p c n", c=NCH)
        b_xn = [Buf("xn%d" % j) for j in range(TBT)]
        ntmp = self.norm_tmp(small=True)
        bst = ar.f32(NSUB * 36).rearrange("p (s b x) -> p s b x", s=NSUB, b=6)
        b_bst = Buf("bst")
        mv = ar.f32(NSUB * 2).rearrange("p (s x) -> p s x", s=NSUB)
        rstd = ar.f32(NSUB)
        b_mv = Buf("mv")
        WS = 4608
        wslot = [ar.f32(WS), ar.f32(WS)]
        b_w = [Buf("sw0"), Buf("sw1")]
        bsbc = [ar.f32(128), ar.f32(128)]
        gl = [ar.f32(512), ar.f32(512)]
        b_gl = [Buf("gl0"), Buf("gl1")]
        vt = ar.f32(384)
        b_vt = Buf("vt")
        vhat = ar.bf16(384)
        b_vh = Buf("vhat")
        vm = ar.f32(512)
        b_vm = Buf("vm")
        gT = ar.bf16(3 * 512).rearrange("p (f n) -> p f n", f=3)
        b_gT = [Buf("gT%d" % f) for f in range(3)]
        R = ar.f32(3 * 128).rearrange("p (f n) -> p f n", f=3)
        b_R = Buf("R")

        def wv1(sl):
            return wslot[sl].bitcast(BF16)[:, 0:8 * 512].rearrange("p (k n) -> p k n", k=8)

        def wu2(sl):
            return wslot[sl].bitcast(BF16)[:, 0:3072].rearrange("p (k n) -> p k n", k=8)

        def wv2(sl):
            return wslot[sl].bitcast(BF16)[:, 3072:6144].rearrange("p (k n) -> p k n", k=8)

        def wo2(sl):
            return wslot[sl].bitcast(BF16)[:, 6144:9216].rearrange("p (f n) -> p f n", f=3)

        loads = []
        for tb in range(SEQ // (TBT * 512)):
            for vb in range(6):
                loads.append((1, tb, vb))
            for g in range(8):
                loads.append((2, tb, g))

        def load_w(idx):
            ph, tb, i = loads[idx]
            sl = idx % 2
            nm = "sw%d" % sl
            if ph == 1:
                S.dma("pool", nm, lambda e: e.dma_start(out=wv1(sl), in_=w_in[:, :, 3072 + i * 512:3072 + (i + 1) * 512]),
                      writes=[b_w[sl]])
            else:
                S.dma("pool", nm, lambda e: e.dma_start(out=wu2(sl), in_=w_in[:, :, i * 384:(i + 1) * 384]),
                      writes=[b_w[sl]])
                S.dma("pool", nm, lambda e: e.dma_start(out=wv2(sl), in_=w_in[:, :, 3072 + i * 384:3072 + (i + 1) * 384]),
                      writes=[b_w[sl]])
                S.dma("pool", nm, lambda e: e.dma_start(out=wo2(sl), in_=w_out[:, 3 * i:3 * i + 3, :]), writes=[b_w[sl]])
                S.dma("pool", nm, lambda e: e.dma_start(out=bsbc[sl], in_=b_s[i].partition_broadcast(128)),
                      writes=[b_w[sl]])

        cnt = {"v": 0, "u": 0, "gl": 0}

        def phase1(tb, vb, sl):
            W = wv1(sl)
            for sub in range(NSUB):
                bk = cnt["v"] % 2
                cnt["v"] += 1
                gi = cnt["gl"] % 2
                cnt["gl"] += 1
                for k in range(8):
                    S.op("pe", lambda e, bk=bk, k=k, sub=sub: e.matmul(
                        self.bank(bk), lhsT=xn[:, k, sub * 128:(sub + 1) * 128], rhs=W[:, k, :],
                        start=(k == 0), stop=(k == 7)), reads=[b_w[sl], b_xn[sub // 4]], writes=[pb[bk]])
                S.op("act", lambda e, bk=bk, gi=gi: e.activation(out=gl[gi], in_=self.bank(bk), func=AF.Gelu),
                     reads=[pb[bk]], writes=[b_gl[gi]])
                S.op("dve", lambda e, gi=gi, sub=sub: e.bn_stats(out=bst[:, sub, vb, :], in_=gl[gi]),
                     reads=[b_gl[gi]], writes=[b_bst])

        def stats():
            for sub in range(NSUB):
                S.op("dve", lambda e, sub=sub: e.bn_aggr(out=mv[:, sub, :],
                                                         in_=bst[:, sub, :, :].rearrange("p b x -> p (b x)")),
                     reads=[b_bst], writes=[b_mv])
            S.op("act", lambda e: e.activation(out=rstd, in_=mv[:, :, 1], func=AF.Ln, scale=1.0, bias=EPS),
                 reads=[b_mv], writes=[b_mv])
            S.op("act", lambda e: e.activation(out=rstd, in_=rstd, func=AF.Exp, scale=-0.5), reads=[b_mv], writes=[b_mv])

        def phase2(tb, g, sl):
            Wu, Wv, Wo = wu2(sl), wv2(sl), wo2(sl)
            bw = b_w[sl]
            S.op("pe", lambda e: e.matmul(self.bank(7)[:, 0:128], lhsT=self.ones_bf, rhs=wct[:, g, :], start=True, stop=True),
                 reads=[self.b_ones, b_wct], writes=[pb[7]])
            for fc in range(3):
                S.op("dve", lambda e, fc=fc: e.scalar_tensor_tensor(
                    out=R[:, fc, :], in0=self.bank(7)[:, 0:128], scalar=self.vecs_sb[:, bcol + 3 * g + fc:bcol + 3 * g + fc + 1],
                    in1=bsbc[sl], op0=ALU.mult, op1=ALU.add), reads=[pb[7], self.b_vecs, bw], writes=[b_R])
            for j in range(TBT):
                t = tb * TBT + j
                tok = slice(t * 512, (t + 1) * 512)
                def vproj(c):
                    sub = j * 4 + c
                    bk = cnt["v"] % 2
                    cnt["v"] += 1
                    for k in range(8):
                        S.op("pe", lambda e, bk=bk, k=k, sub=sub: e.matmul(
                            self.bank(bk)[:, 0:384], lhsT=xn[:, k, sub * 128:(sub + 1) * 128], rhs=Wv[:, k, :],
                            start=(k == 0), stop=(k == 7)), reads=[bw, b_xn[j]], writes=[pb[bk]])
                    S.op("act", lambda e, bk=bk: e.activation(out=vt, in_=self.bank(bk)[:, 0:384], func=AF.Gelu),
                         reads=[pb[bk]], writes=[b_vt])
                    S.op("dve", lambda e, sub=sub: e.tensor_scalar(out=vhat, in0=vt, scalar1=mv[:, sub, 0:1],
                                                                   scalar2=rstd[:, sub:sub + 1], op0=ALU.subtract,
                                                                   op1=ALU.mult), reads=[b_vt, b_mv], writes=[b_vh])

                def vmix(c):
                    for fc in range(3):
                        S.op("pe", lambda e, fc=fc, c=c: e.matmul(
                            self.bank(2 + fc)[:, c * 128:(c + 1) * 128], lhsT=vhat[:, fc * 128:(fc + 1) * 128],
                            rhs=wct[:, g, :], start=True, stop=True), reads=[b_vh, b_wct], writes=[pb[2 + fc]])

                vproj(0)
                for c in range(4):
                    vmix(c)
                    if c + 1 < 4:
                        vproj(c + 1)
                for fc in range(3):
                    bk = 5 + cnt["u"] % 2
                    cnt["u"] += 1
                    gi = cnt["gl"] % 2
                    cnt["gl"] += 1
                    for k in range(8):
                        S.op("pe", lambda e, bk=bk, k=k, fc=fc, j=j: e.matmul(
                            self.bank(bk), lhsT=Wu[:, k, fc * 128:(fc + 1) * 128], rhs=xn[:, k, j * 512:(j + 1) * 512],
                            start=(k == 0), stop=(k == 7)), reads=[bw, b_xn[j]], writes=[pb[bk]])
                    S.op("act", lambda e, bk=bk, gi=gi: e.activation(out=gl[gi], in_=self.bank(bk), func=AF.Gelu),
                         reads=[pb[bk]], writes=[b_gl[gi]])
                    S.op("dve", lambda e, fc=fc: e.scalar_tensor_tensor(
                        out=vm.rearrange("p (c n) -> p c n", c=4), in0=self.bank(2 + fc).rearrange("p (c n) -> p c n", c=4),
                        scalar=self.vecs_sb[:, gcol + 3 * g + fc:gcol + 3 * g + fc + 1],
                        in1=R[:, fc, :].unsqueeze(1).broadcast_to([128, 4, 128]), op0=ALU.mult, op1=ALU.add),
                        reads=[pb[2 + fc], self.b_vecs, b_R], writes=[b_vm])
                    S.op("dve", lambda e, fc=fc, gi=gi: e.tensor_tensor(out=gT[:, fc, :], in0=vm, in1=gl[gi], op=ALU.mult),
                         reads=[b_vm, b_gl[gi]], writes=[b_gT[fc]])
                for dc in range(NCH):
                    yb = (7, 2, 3, 4)[dc % 4]
                    for fc in range(3):
                        S.op("pe", lambda e, dc=dc, fc=fc, yb=yb: e.matmul(
                            self.bank(yb), lhsT=Wo[:, fc, dc * 128:(dc + 1) * 128], rhs=gT[:, fc, :],
                            start=(fc == 0), stop=(fc == 2)), reads=[bw, b_gT[fc]], writes=[pb[yb]])
                    S.op("dve", lambda e, dc=dc, tok=tok, yb=yb: e.tensor_tensor(out=self.X[:, dc, tok], in0=self.bank(yb),
                                                                                 in1=self.X[:, dc, tok], op=ALU.add),
                         reads=[pb[yb], self.xb[dc][t]], writes=[self.xb[dc][t]])

        load_w(0)
        for idx, (ph, tb, i) in enumerate(loads):
            if idx + 1 < len(loads):
                load_w(idx + 1)
            if ph == 1 and i == 0:
                self.emit_norm(lp + "_mix_norm", tb * TBT, TBT, xn, b_xn, ntmp)
            if ph == 1:
                phase1(tb, i, idx % 2)
                if i == 5:
                    stats()
            else:
                phase2(tb, i, idx % 2)


FULL_PLAN = []
for _l in range(4):
    FULL_PLAN.append(("ffn", "l%d_ffn1" % _l))
    FULL_PLAN.append((["gla", "sgu", "swa", "gla"][_l], "l%d" % _l))
    FULL_PLAN.append(("ffn", "l%d_ffn2" % _l))


def needed_weights(plan, final_norm, inputs):
    names = []
    for item in plan:
        if item[0] == "ffn":
            names += [item[1] + s for s in ("_norm", "_w_in", "_w_out")]
        else:
            lp = item[1]
            names.append(lp + "_mix_norm")
            names += [k for k in inputs if k.startswith(lp + "_" + item[0] + "_")]
    if final_norm:
        names.append("final_norm")
    return names


def run_plan(inputs, plan, final_norm, x_override=None, trace=False):
    names = needed_weights(plan, final_norm, inputs)
    shapes = {n: tuple(np.asarray(inputs[n]).shape) for n in names}
    import time as _t
    _t0 = _t.time()
    prog = Prog(plan, final_norm, shapes)
    nc = prog.build()
    print("[kernel] build %.1fs, instr: %s" % (_t.time() - _t0, {k: len(q.items) for k, q in prog.S.q.items()}), flush=True)
    x = np.asarray(inputs["x"] if x_override is None else x_override, dtype=np.float32)
    pos = np.asarray(inputs["positions"]).astype(np.int32)
    w = {n: np.ascontiguousarray(np.asarray(inputs[n], dtype=np.float32)) for n in names}
    in_maps = []
    for b in range(N_CORES):
        m = {"x": np.ascontiguousarray(x[b]), "pos": np.ascontiguousarray(pos[b])}
        m.update(w)
        in_maps.append(m)
    _t0 = _t.time()
    res = run_bass_kernel_spmd(nc, in_maps, core_ids=list(range(N_CORES)), trace=trace)
    print("[kernel] run %.1fs" % (_t.time() - _t0), flush=True)
    out = np.stack([r["y"] for r in res.results], axis=0)
    return out, res


def kernel(**inputs):
    out, _ = run_plan(inputs, FULL_PLAN, True)
    return out
```
